# Optimizing a Trainium2 kernel written in Bass

```python
import math
import jax, jax.numpy as jnp
from jax import lax
import numpy as np

D_MODEL = 4096
BATCH = 4
SEQ = 2048
DEPTH = 4

CONV_WIDTH = D_MODEL // 4
CONV_K = 3
HEAD_DIM = 128
GROUPS = ((128, 1), (512, 4), (2048, 16))
HEADS_PER_GROUP = 8
N_ATTN_HEADS = HEADS_PER_GROUP * len(GROUPS)
ATTN_WIDTH = N_ATTN_HEADS * HEAD_DIM
MERGED_ATTN_WIDTH = HEADS_PER_GROUP * HEAD_DIM
BLOCK = 128
NUM_BUCKETS = 32
MAX_DISTANCE = 2048
D_FF = 2 * D_MODEL
EPS = 1e-6
NEG = -1e30

SPLIT_SIZES = (CONV_WIDTH, CONV_WIDTH, CONV_WIDTH, ATTN_WIDTH, ATTN_WIDTH, ATTN_WIDTH, D_MODEL, D_MODEL)
IN_COLS = sum(SPLIT_SIZES)
SPLIT_POINTS = tuple(int(c) for c in np.cumsum(SPLIT_SIZES)[:-1])

kernel_name = "hybrid_conv_dilated_attn_gated_trunk"


def rmsnorm(x, g):
    xf = x.astype(jnp.float32)
    inv = lax.rsqrt(jnp.mean(xf * xf, axis=-1, keepdims=True) + EPS)
    return (xf * inv * g.astype(jnp.float32)).astype(x.dtype)


def causal_dwconv3(u, w, b):
    S = u.shape[1]
    up = jnp.pad(u, ((0, 0), (CONV_K - 1, 0), (0, 0)))
    return up[:, :S] * w[0] + up[:, 1:S + 1] * w[1] + up[:, 2:S + 2] * w[2] + b


def t5_bucket(dist):
    max_exact = NUM_BUCKETS // 2
    distf = jnp.maximum(dist, 1).astype(jnp.float32)
    large = max_exact + (jnp.log(distf / max_exact) / math.log(MAX_DISTANCE / max_exact)
                         * (NUM_BUCKETS - max_exact)).astype(jnp.int32)
    large = jnp.minimum(large, NUM_BUCKETS - 1)
    return jnp.where(dist < max_exact, dist, large)


def dilated_window_attention(q, k, v, table_g, window, dilation):
    Bn, H, S, hd = q.shape
    steps = window // dilation
    assert steps <= BLOCK
    L = S // dilation
    nb = -(-L // BLOCK)
    Lp = nb * BLOCK

    def to_sub(t):
        t = t.reshape(Bn, H, L, dilation, hd).transpose(0, 1, 3, 2, 4)
        return jnp.pad(t, ((0, 0), (0, 0), (0, 0), (0, Lp - L), (0, 0)))

    qs, ks, vs = to_sub(q), to_sub(k), to_sub(v)
    qb = qs.reshape(Bn, H, dilation, nb, BLOCK, hd)

    def band(t):
        tp = jnp.pad(t, ((0, 0), (0, 0), (0, 0), (BLOCK, 0), (0, 0)))
        prev = tp[:, :, :, :Lp].reshape(Bn, H, dilation, nb, BLOCK, hd)
        cur = t.reshape(Bn, H, dilation, nb, BLOCK, hd)
        return jnp.concatenate([prev, cur], axis=-2)

    kb, vb = band(ks), band(vs)
    s = jnp.einsum('bhrnqd,bhrnkd->bhrnqk', qb, kb).astype(jnp.float32) * (HEAD_DIM ** -0.5)

    q_idx = jnp.arange(BLOCK)
    k_idx = jnp.arange(2 * BLOCK)
    delta = (q_idx[:, None] + BLOCK) - k_idx[None, :]
    in_band = (delta >= 0) & (delta <= steps)
    bias_delta = table_g[t5_bucket(jnp.arange(steps + 1) * dilation)].astype(jnp.float32)
    bias_blk = bias_delta[jnp.clip(delta, 0, steps)].transpose(2, 0, 1)
    not_front_pad = (jnp.arange(nb)[:, None, None] > 0) | (k_idx[None, None, :] >= BLOCK)
    valid = in_band[None] & not_front_pad

    s = s + bias_blk[None, :, None, None]
    s = jnp.where(valid[None, None, None], s, NEG)
    m = jnp.max(s, axis=-1, keepdims=True)
    p = jnp.exp(s - m)
    den = jnp.sum(p, axis=-1, keepdims=True)
    o = jnp.einsum('bhrnqk,bhrnkd->bhrnqd', p, vb.astype(jnp.float32)) / den
    lse = (m + jnp.log(den))[..., 0]

    o = o.reshape(Bn, H, dilation, Lp, hd)[:, :, :, :L].transpose(0, 1, 3, 2, 4).reshape(Bn, H, S, hd)
    lse = lse.reshape(Bn, H, dilation, Lp)[:, :, :, :L].transpose(0, 1, 3, 2).reshape(Bn, H, S)
    return o, lse


def mixed_dilated_attention(q, k, v, rel_bias_table):
    Bn, S, _ = q.shape
    def heads(t):
        return t.reshape(Bn, S, N_ATTN_HEADS, HEAD_DIM).transpose(0, 2, 1, 3)
    qh, kh, vh = heads(q), heads(k), heads(v)
    outs, lses = [], []
    for g, (window, dilation) in enumerate(GROUPS):
        sl = slice(g * HEADS_PER_GROUP, (g + 1) * HEADS_PER_GROUP)
        o, lse = dilated_window_attention(qh[:, sl], kh[:, sl], vh[:, sl],
                                          rel_bias_table[:, sl], window, dilation)
        outs.append(o)
        lses.append(lse)
    o = jnp.stack(outs, 0)
    w = jax.nn.softmax(jnp.stack(lses, 0), axis=0)
    o = jnp.sum(w[..., None] * o, axis=0)
    return o.transpose(0, 2, 1, 3).reshape(Bn, S, MERGED_ATTN_WIDTH).astype(q.dtype)


def setup_inputs(seed: int = 0) -> dict:
    key = jax.random.key(seed)
    ks = jax.random.split(key, 15)
    f32 = jnp.float32
    n = lambda k, shape, scale: jax.random.normal(k, shape, f32) * scale
    return {
        "x": n(ks[0], (BATCH, SEQ, D_MODEL), 1.0),
        "rel_bias_table": n(ks[1], (NUM_BUCKETS, N_ATTN_HEADS), 0.5),
        "norm_mix_g": 1.0 + n(ks[2], (DEPTH, D_MODEL), 0.05),
        "w_in": n(ks[3], (DEPTH, D_MODEL, IN_COLS), D_MODEL ** -0.5),
        "conv_a_w": n(ks[4], (DEPTH, CONV_K, CONV_WIDTH), CONV_K ** -0.5),
        "conv_a_b": n(ks[5], (DEPTH, CONV_WIDTH), 0.02),
        "w_branch_a": n(ks[6], (DEPTH, CONV_WIDTH, D_MODEL), CONV_WIDTH ** -0.5),
        "w_branch_b": n(ks[7], (DEPTH, MERGED_ATTN_WIDTH, D_MODEL), MERGED_ATTN_WIDTH ** -0.5),
        "w_o": n(ks[8], (DEPTH, D_MODEL, D_MODEL), D_MODEL ** -0.5),
        "norm_ffn_g": 1.0 + n(ks[9], (DEPTH, D_MODEL), 0.05),
        "w_up": n(ks[10], (DEPTH, D_MODEL, 2 * D_FF), D_MODEL ** -0.5),
        "conv_f_w": n(ks[11], (DEPTH, CONV_K, D_FF), CONV_K ** -0.5),
        "conv_f_b": n(ks[12], (DEPTH, D_FF), 0.02),
        "w_down": n(ks[13], (DEPTH, D_FF, D_MODEL), D_FF ** -0.5),
        "norm_final_g": 1.0 + n(ks[14], (D_MODEL,), 0.05),
    }


def reference(x, rel_bias_table, norm_mix_g, w_in, conv_a_w, conv_a_b, w_branch_a,
              w_branch_b, w_o, norm_ffn_g, w_up, conv_f_w, conv_f_b, w_down, norm_final_g):
    for l in range(DEPTH):
        h = rmsnorm(x, norm_mix_g[l])
        proj = h @ w_in[l]
        a_h, a_b, a_c, q, k, v, gate_a, gate_b = jnp.split(proj, SPLIT_POINTS, axis=-1)
        y_a = a_b * causal_dwconv3(a_c * a_h, conv_a_w[l], conv_a_b[l])
        branch_a = y_a @ w_branch_a[l]
        y_b = mixed_dilated_attention(q, k, v, rel_bias_table)
        branch_b = y_b @ w_branch_b[l]
        merged = jax.nn.sigmoid(gate_a) * branch_a + jax.nn.sigmoid(gate_b) * branch_b
        x = x + merged @ w_o[l]
        h = rmsnorm(x, norm_ffn_g[l])
        a, b = jnp.split(h @ w_up[l], 2, axis=-1)
        a = causal_dwconv3(a, conv_f_w[l], conv_f_b[l])
        x = x + (jax.nn.gelu(a) * b) @ w_down[l]
    return rmsnorm(x, norm_final_g)
```

```python
import math
from contextlib import ExitStack

import numpy as np
import concourse.bass as bass
import concourse.mybir as mybir
from concourse.bass_utils import run_bass_kernel_spmd

F32 = mybir.dt.float32
BF16 = mybir.dt.bfloat16
AF = mybir.ActivationFunctionType
ALU = mybir.AluOpType

D = 4096
T = 1024
TH = 1026
NCH = 32
DEPTH = 4
NHEAD = 24
IN_COLS = 20480
DFF = 8192
EPS = 1e-6
NEG = -30000.0
GROUP_DIL = (1, 4, 16)
N_CORES = 8

PL = 352
P_GMIX, P_GFFN, P_CAW, P_CAB, P_CFW, P_CFB = 0, 32, 64, 88, 96, 288
P_GFIN = DEPTH * PL
P_FLAG = P_GFIN + 32
P_EPS = P_FLAG + 1
NPAR = P_EPS + 7

NW = 4


class Sem:
    def __init__(self, nc, name, step):
        self.name = name
        self.h = nc.alloc_semaphore(name)
        self.step = step
        self.count = 0


class Builder:
    def __init__(self, depth=DEPTH, level=99):
        self.depth = depth
        self.level = level
        self.nc = bass.Bass("TRN2", target_bir_lowering=False)
        self.q = {e: [] for e in ("pe", "act", "dve", "pool", "sp")}
        self.waited = {e: {} for e in self.q}
        self.nsem = 0

    def sem(self, name, step):
        self.nsem += 1
        return Sem(self.nc, name, step)

    def op(self, eng, fn, waits=(), inc=None):
        w = []
        flat = []

        def fl(x):
            if x is None:
                return
            if isinstance(x, tuple) and len(x) == 2 and isinstance(x[0], Sem):
                flat.append(x)
                return
            for y in x:
                fl(y)
        fl(list(waits))
        for t in flat:
            s, v = t
            if self.waited[eng].get(s.name, 0) >= v:
                continue
            self.waited[eng][s.name] = v
            w.append((s, v))
        tok = None
        if inc is not None:
            inc.count += inc.step
            tok = (inc, inc.count)
        self.q[eng].append((w, fn, inc))
        return tok

    def cop(self, eng, fn, waits=()):
        return self.op(eng, fn, waits, self.P[eng])

    def replay(self, eng, e):
        for (w, fn, inc) in self.q[eng]:
            for (s, v) in w:
                e.wait_ge(s.h, v)
            ins = fn(e)
            if inc is not None:
                ins.then_inc(inc.h, inc.step)

    def build(self):
        nc = self.nc
        L = self.depth
        dt = nc.dram_tensor
        self.xT = dt("xT", [D, T], F32, kind="ExternalInput")
        self.xh = dt("xh", [128, 64], F32, kind="ExternalInput")
        self.params = dt("params", [128, NPAR], F32, kind="ExternalInput")
        self.biasT = dt("biasT", [NHEAD * 128, 512], F32, kind="ExternalInput")
        self.ident = dt("ident", [128, 128], F32, kind="ExternalInput")
        self.w_in = dt("w_in", [L, D, IN_COLS], F32, kind="ExternalInput")
        if self.level >= 7:
            self.w_ba = dt("w_branch_a", [L, 1024, D], F32, kind="ExternalInput")
            self.w_bb = dt("w_branch_b", [L, 1024, D], F32, kind="ExternalInput")
            self.w_o = dt("w_o", [L, D, D], F32, kind="ExternalInput")
        if self.level >= 9:
            self.w_up = dt("w_up", [L, D, 2 * DFF], F32, kind="ExternalInput")
            self.w_down = dt("w_down", [L, DFF, D], F32, kind="ExternalInput")
        self.outT = dt("outT", [D, T], F32, kind="ExternalOutput")
        if self.level == 6:
            self.dbg = dt("dbg", [128, 16 * T], BF16, kind="ExternalOutput")
        self.xs = dt("xs", [D, T], F32)
        self.q_scr = dt("q_scr", [NHEAD * 128, T], BF16)
        self.ks = [dt(f"ks{i}", [1024, T], BF16) for i in range(6)]
        self.kr = [[dt(f"kr{p}_{i}", [2048, T], BF16) for i in range(6)] for p in range(2)]
        self.hsend = dt("hsend", [128, 64], F32)
        self.hrecv = [dt(f"hrecv{i}", [256, 64], F32) for i in range(2)]

        sb = nc.alloc_sbuf_tensor
        self.A = sb("A", [128, NCH, TH], BF16)
        self.B = sb("B", [128, 16, T], BF16)
        self.W = [sb(f"W{i}", [128, NCH, 128], BF16) for i in range(NW)]
        self.YA = sb("YA", [128, 8, T], BF16)
        self.YB = sb("YB", [128, 8, T], BF16)
        self.XC = [sb(f"XC{i}", [128, TH], F32) for i in range(3)]
        self.ST = [sb(f"ST{i}", [128, TH], BF16) for i in range(3)]
        self.INV = sb("INV", [128, TH], F32)
        self.PAR = sb("PAR", [128, NPAR], F32)
        self.ONES = sb("ONES", [128, 128], BF16)
        self.IDENT = sb("IDENT", [128, 128], BF16)
        self.HX = sb("HX", [128, 64], F32)
        self.HSEND = sb("HSEND", [128, 64], F32)
        self.U = sb("U", [128, TH], F32)
        self.AHh = sb("AHh", [128, 8, 2], F32)
        self.SB = [sb(f"SB{i}", [128, 2, 128], F32) for i in range(2)]
        self.PT = [sb(f"PT{i}", [128, 2, 128], BF16) for i in range(2)]
        self.VST = [sb(f"VST{i}", [128, 8, 128], BF16) for i in range(2)]
        Bflat = self.B[:].rearrange("p c t -> p (c t)")
        self.AH = Bflat.bitcast(F32).rearrange("p (c t) -> p c t", c=8)
        self.HT = [[Bflat[:, (s * 6 + i) * 1024:(s * 6 + i + 1) * 1024] for i in range(6)] for s in range(2)]
        self.Oacc = Bflat[:, 12288:14336].bitcast(F32)
        self.Dacc = Bflat[:, 14336:16384].bitcast(F32)
        self.Rcp = self.U[:, 0:T]

        ps = nc.alloc_psum_tensor
        self.ACC = [ps(f"acc{i}", [128, 2, 512], F32) for i in range(2)]
        self.HB = ps("hb", [128, 512], F32)
        self.SA = ps("sa", [128, 2, 512], F32)
        self.VT = ps("vt", [128, 8, 128], BF16)

        self.P = {e: self.sem(f"P_{e}", 1) for e in ("pe", "act", "dve")}
        self.s_w = [self.sem(f"w{i}", 16) for i in range(NW)]
        self.s_xl = [self.sem(f"xl{i}", 16) for i in range(3)]
        self.s_xs = [self.sem(f"xs{i}", 16) for i in range(3)]
        self.s_st = [self.sem(f"st{i}", 16) for i in range(3)]
        self.s_vst = [self.sem(f"vst{i}", 16) for i in range(2)]
        self.s_ht = [self.sem(f"ht{i}", 16) for i in range(2)]
        self.s_cc = self.sem("cc", 1)
        self.s_misc = self.sem("misc", 16)
        self.s_hs = self.sem("hs", 16)
        self.s_hx = self.sem("hx", 16)

        self.wtile = 0
        self.wslot_free = [None] * NW
        self.acc_i = 0
        self.acc_free = [None, None]
        self.xc_i = 0
        self.xc_free = [None] * 3
        self.st_i = 0
        self.st_free = [None] * 3
        self.xs_store = [None] * NCH
        self.hx_tok = None
        self.hx_free = None
        self.hsend_free = None
        self.cc_n = 0
        self.sa_free = None
        self.u_free = None
        self.b_free = None
        self.a_free = None

        self.setup()
        for l in range(L):
            self.layer(l)
        self.final_norm()

        with nc.Block() as block:
            @block.tensor
            def _(e):
                self.replay("pe", e)

            @block.scalar
            def _(e):
                self.replay("act", e)

            @block.vector
            def _(e):
                self.replay("dve", e)

            @block.gpsimd
            def _(e):
                self.replay("pool", e)

            @block.sync
            def _(e):
                self.replay("sp", e)
        return nc

    def par(self, i):
        return self.PAR[:, i:i + 1]

    def latest(self):
        return [(s, s.count) for s in self.P.values() if s.count > 0]

    def setup(self):
        t = []
        t.append(self.op("sp", lambda e: e.dma_start(out=self.PAR[:], in_=self.params[:, :]), inc=self.s_misc))
        t.append(self.op("sp", lambda e: e.dma_start(out=self.HX[:], in_=self.xh[:, :]), inc=self.s_misc))
        t.append(self.op("sp", lambda e: e.dma_start(out=self.xs[:, :], in_=self.xT[:, :]), inc=self.s_misc))
        t.append(self.op("pool", lambda e: e.dma_start(out=self.IDENT[:], in_=self.ident[:, :]), inc=self.s_misc))
        self.setup_tok = t[-1]
        self.hx_tok = self.setup_tok
        self.cop("dve", lambda e: e.memset(self.ONES[:], 1.0))

    def gemm_chunk(self, w_ap, kc, rhs_main, rhs_halo, rhs_waits, epilogue):
        slot = self.wtile % NW
        self.wtile += 1
        Wt = self.W[slot]
        src = w_ap.rearrange("(kc p) n -> p kc n", p=128)
        ld = self.op("pool", lambda e: e.dma_start(out=Wt[:, 0:kc, :], in_=src),
                     waits=[self.wslot_free[slot]], inc=self.s_w[slot])
        a = self.acc_i % 2
        self.acc_i += 1
        acc = self.ACC[a]
        hslot = self.SA[:, a, 0:2]

        def pe_fn(e):
            ins = None
            for k in range(kc):
                for h in range(2):
                    ins = e.matmul(acc[:, h, :], lhsT=Wt[:, k, :], rhs=rhs_main(k, h), start=(k == 0), stop=(k == kc - 1))
                if rhs_halo is not None:
                    ins = e.matmul(hslot, lhsT=Wt[:, k, :], rhs=rhs_halo(k), start=(k == 0), stop=(k == kc - 1))
            return ins

        mm = self.cop("pe", pe_fn, waits=[ld, self.acc_free[a], self.sa_free if rhs_halo is not None else None] + list(rhs_waits))
        self.wslot_free[slot] = mm
        accv = acc[:].rearrange("p a b -> p (a b)")
        self.acc_free[a] = epilogue(accv, hslot, mm)
        return mm

    def srcA(self):
        return (lambda k, h: self.A[:, k, h * 512:(h + 1) * 512]), (lambda k: self.A[:, k, T:TH])

    def norm(self, gbase, final=False):
        pre = self.latest()
        ssm = self.SA
        ssh = self.HB[:, 4:6]
        n = NCH
        loads = {}

        def load(c, extra=()):
            s = self.xc_i % 3
            self.xc_i += 1
            tok = self.op("sp", lambda e: e.dma_start(out=self.XC[s][:, 0:T], in_=self.xs[c * 128:(c + 1) * 128, :]),
                          waits=[self.xc_free[s], self.xs_store[c], self.setup_tok] + list(extra), inc=self.s_xl[s])
            loads[c] = (s, tok)

        for c in range(min(3, n)):
            load(c)
        pe_last = None
        for c in range(n):
            s, ltok = loads.pop(c)
            t = self.st_i % 3
            self.st_i += 1
            X = self.XC[s]
            S_ = self.ST[t]
            self.cop("act", lambda e, X=X, c=c: e.activation(out=X[:, T:TH], in_=self.HX[:, 2 * c:2 * c + 2], func=AF.Identity),
                     waits=[ltok, self.hx_tok])
            sq = self.cop("act", lambda e, X=X, S_=S_: e.activation(out=S_[:, :], in_=X[:, :], func=AF.Square),
                          waits=[self.st_free[t], (self.P["act"], self.P["act"].count)])
            self.xc_free[s] = sq

            def pe_fn(e, S_=S_, c=c):
                for h in range(2):
                    e.matmul(ssm[:, h, :], lhsT=self.ONES[:], rhs=S_[:, h * 512:(h + 1) * 512], start=(c == 0), stop=(c == n - 1))
                return e.matmul(ssh, lhsT=self.ONES[:], rhs=S_[:, T:TH], start=(c == 0), stop=(c == n - 1))
            w = [sq]
            if c == 0:
                w += pre + [self.sa_free, self.acc_free[0], self.acc_free[1]]
            pe_last = self.cop("pe", pe_fn, waits=w)
            self.st_free[t] = pe_last
            if c + 3 < n:
                load(c + 3)
        a1 = self.cop("act", lambda e: e.activation(out=self.INV[:, 0:T], in_=ssm[:].rearrange("p a b -> p (a b)"), func=AF.Sqrt,
                                                    bias=self.par(P_EPS), scale=1.0 / D), waits=[pe_last] + pre)
        a2 = self.cop("act", lambda e: e.activation(out=self.INV[:, T:TH], in_=ssh, func=AF.Sqrt,
                                                    bias=self.par(P_EPS), scale=1.0 / D), waits=[pe_last, a1])
        self.sa_free = a2
        inv = self.cop("dve", lambda e: e.reciprocal(out=self.INV[:, :], in_=self.INV[:, :]), waits=[a1, a2] + pre)
        for c in range(min(3, n)):
            load(c)
        last = None
        out_tok = []
        for c in range(n):
            s, ltok = loads.pop(c)
            X = self.XC[s]
            g = self.par(gbase + c)
            if not final:
                hc = self.cop("dve", lambda e, X=X, c=c: e.tensor_copy(out=X[:, T:TH], in_=self.HX[:, 2 * c:2 * c + 2]),
                              waits=[ltok, self.hx_tok])
                last = self.cop("dve", lambda e, X=X, c=c, g=g: e.scalar_tensor_tensor(
                    out=self.A[:, c, :], in0=X[:, :], scalar=g, in1=self.INV[:, :], op0=ALU.mult, op1=ALU.mult),
                    waits=[hc, inv, self.a_free])
                self.xc_free[s] = last
            else:
                o = self.cop("dve", lambda e, X=X, g=g: e.scalar_tensor_tensor(
                    out=X[:, 0:T], in0=X[:, 0:T], scalar=g, in1=self.INV[:, 0:T], op0=ALU.mult, op1=ALU.mult),
                    waits=[ltok, inv])
                st = self.op("sp", lambda e, X=X, c=c: e.dma_start(out=self.outT[c * 128:(c + 1) * 128, :], in_=X[:, 0:T]),
                             waits=[o], inc=self.s_xs[s])
                self.xc_free[s] = st
                out_tok.append(st)
            if c + 3 < n:
                load(c + 3)
        self.hx_free = last
        if final:
            fin = [(sm, sm.count) for sm in self.s_xs]
            self.op("sp", lambda e: None, waits=fin)
        return last

    def rmw_gemm(self, w_rows_ap, kc, src, src_waits, last_pass):
        loads = {}

        def load(c):
            s = self.xc_i % 3
            self.xc_i += 1
            tok = self.op("sp", lambda e: e.dma_start(out=self.XC[s][:, 0:T], in_=self.xs[c * 128:(c + 1) * 128, :]),
                          waits=[self.xc_free[s], self.xs_store[c], self.setup_tok], inc=self.s_xl[s])
            loads[c] = (s, tok)

        load(0)
        load(1)
        for c in range(NCH):
            s, ltok = loads.pop(c)
            X = self.XC[s]

            def epi(accv, hslot, mm, X=X, c=c, s=s, ltok=ltok):
                waits = [mm, ltok]
                if last_pass and c == 0:
                    waits.append(self.hsend_free)
                ad = self.cop("dve", lambda e: e.tensor_tensor(out=X[:, 0:T], in0=accv, in1=X[:, 0:T], op=ALU.add), waits=waits)
                fin = ad
                if last_pass:
                    fin = self.cop("dve", lambda e: e.tensor_copy(out=self.HSEND[:, 2 * c:2 * c + 2], in_=X[:, T - 2:T]), waits=[ad])
                st = self.op("sp", lambda e: e.dma_start(out=self.xs[c * 128:(c + 1) * 128, :], in_=X[:, 0:T]),
                             waits=[fin], inc=self.s_xs[s])
                self.xc_free[s] = st
                self.xs_store[c] = st
                self.rmw_last = fin
                return ad

            self.gemm_chunk(w_rows_ap[:, c * 128:(c + 1) * 128], kc, src, None, src_waits, epi)
            if c + 2 < NCH:
                load(c + 2)

    def kv_collective(self, par, i, toks):
        if self.level < 3:
            return None
        return self.op("pool", lambda e: e.collective_compute(
            "AllGather", ALU.bypass, replica_groups=[[0, 1], [2, 3], [4, 5], [6, 7]],
            ins=[self.ks[i].ap().opt()], outs=[self.kr[par][i].ap().opt()]), waits=toks, inc=self.s_cc)

    def halo_exchange(self):
        i = self.cc_n % 2
        self.cc_n += 1
        d = self.op("sp", lambda e: e.dma_start(out=self.hsend[:, :], in_=self.HSEND[:]), waits=[self.rmw_last], inc=self.s_hs)
        self.hsend_free = d
        cc = self.op("pool", lambda e: e.collective_compute(
            "AllGather", ALU.bypass, replica_groups=[[0, 1], [2, 3], [4, 5], [6, 7]],
            ins=[self.hsend.ap().opt()], outs=[self.hrecv[i].ap().opt()]), waits=[d], inc=self.s_cc)
        self.hx_tok = self.op("sp", lambda e: e.dma_start(out=self.HX[:], in_=self.hrecv[i][0:128, :]),
                              waits=[cc, self.hx_free], inc=self.s_hx)

    @staticmethod
    def tsub(X, g, j):
        if g == 0:
            return X[:, 128 * j:128 * (j + 1)]
        if g == 1:
            r, n = j // 2, j % 2
            st = r + 512 * n
            return X[:, st:st + 509:4]
        return X[:, j:j + 1017:8]

    @staticmethod
    def like(Y, g):
        return Y

    def att_load(self, i, gh, kv_recv, waits):
        g, h = gh
        hd = 8 * g + h
        s = i % 2
        tl = self.HT[s]
        rows = slice(hd * 128, (hd + 1) * 128)
        hr = slice(h * 128, (h + 1) * 128)
        self.op("sp", lambda e: e.dma_start(out=tl[0], in_=self.q_scr[rows, :]), waits=[self.ht_free[s]] + waits, inc=self.s_ht[s])
        self.op("sp", lambda e: e.dma_start(out=tl[1], in_=self.ks[g][hr, :]), inc=self.s_ht[s])
        self.op("sp", lambda e: e.dma_start(out=tl[2], in_=kv_recv[g][hr, :]), inc=self.s_ht[s])
        self.op("sp", lambda e: e.dma_start(out=tl[3], in_=self.ks[3 + g][hr, :]), inc=self.s_ht[s])
        self.op("sp", lambda e: e.dma_start(out=tl[4], in_=kv_recv[3 + g][hr, :]), inc=self.s_ht[s])
        tok = self.op("sp", lambda e: e.dma_start(out=tl[5].bitcast(F32), in_=self.biasT[rows, :]), inc=self.s_ht[s])
        return (s, tok)

    def att_s(self, g, b, kcur, kprev, qj, bias2, ltok, extra):
        sps = self.SA[:, b, 0:256].rearrange("p (a q) -> p a q", a=2)
        o0 = self.like(sps[:, 0, :], g)
        o1 = self.like(sps[:, 1, :], g)

        def pe_fn(e):
            e.matmul(o0, lhsT=kcur, rhs=qj, start=True, stop=True)
            return e.matmul(o1, lhsT=kprev, rhs=qj, start=True, stop=True)
        sm = self.cop("pe", pe_fn, waits=[ltok, self.sps_free[b], extra])
        S_ = self.SB[b]
        ad = self.cop("dve", lambda e: e.tensor_tensor(out=S_[:], in0=sps, in1=bias2, op=ALU.add), waits=[sm, self.sb_free[b], ltok])
        self.sps_free[b] = ad
        P_ = self.PT[b]
        ex = self.cop("act", lambda e: e.activation(out=P_[:], in_=S_[:], func=AF.Exp), waits=[ad, self.pt_free[b]])
        self.sb_free[b] = ex
        return ex

    def att_pv(self, g, j, b, ex, vcur, vprev):
        ops = self.ACC[0][:, b, 0:128]
        dps = self.ACC[0][:, b, 128:256]
        P_ = self.PT[b]

        def pe_fn(e):
            e.matmul(ops, lhsT=vcur, rhs=P_[:, 0, :], start=True, stop=False)
            e.matmul(ops, lhsT=vprev, rhs=P_[:, 1, :], start=False, stop=True)
            e.matmul(dps, lhsT=self.ONES[:], rhs=P_[:, 0, :], start=True, stop=False)
            return e.matmul(dps, lhsT=self.ONES[:], rhs=P_[:, 1, :], start=False, stop=True)
        pv = self.cop("pe", pe_fn, waits=[ex, self.ops_free[b]])
        self.pt_free[b] = pv
        Oc = self.tsub(self.Oacc, g, j)
        Dc = self.tsub(self.Dacc, g, j)
        opsv = self.like(ops, g)
        dpsv = self.like(dps, g)
        if g == 0:
            o1 = self.cop("dve", lambda e: e.tensor_copy(out=Oc, in_=opsv), waits=[pv, self.od_last])
            o2 = self.cop("dve", lambda e: e.tensor_copy(out=Dc, in_=dpsv), waits=[pv, o1])
            self.ops_free[b] = o2
            self.od_new = [o2]
        else:
            o1 = self.cop("dve", lambda e: e.tensor_tensor(out=Oc, in0=opsv, in1=Oc, op=ALU.add), waits=[pv, self.od_last])
            o2 = self.cop("dve", lambda e: e.tensor_tensor(out=Dc, in0=dpsv, in1=Dc, op=ALU.add), waits=[pv, o1])
            self.ops_free[b] = o2
            self.od_new = [o2]
        return pv

    def att_head(self, g, h, s, ltok, extra):
        Qt, Ko, Kp, Vo, Vp, Bt = self.HT[s]
        Vo3 = Vo.rearrange("p (j d) -> p j d", j=8)
        Vp3 = Vp.rearrange("p (j d) -> p j d", j=8)
        Bt4 = Bt.bitcast(F32).rearrange("p (a q) -> p a q", a=4)
        pend = []
        last_pv = None
        acc_toks = []
        for j in range(8):
            u = self.unit
            self.unit += 1
            b = u % 2
            if g == 0:
                kprev, vprev, boff = (self.tsub(Ko, 0, j - 1), Vo3[:, j - 1, :], 0) if j >= 1 else (self.tsub(Kp, 0, 7), Vp3[:, 7, :], 2)
            elif g == 1:
                kprev, vprev, boff = (self.tsub(Ko, 1, j - 1), Vo3[:, j - 1, :], 0) if j % 2 == 1 else (self.tsub(Kp, 1, j + 1), Vp3[:, j + 1, :], 2)
            else:
                kprev, vprev, boff = self.tsub(Kp, 2, j), Vp3[:, j, :], 2
            kcur, vcur = self.tsub(Ko, g, j), Vo3[:, j, :]
            qj = self.tsub(Qt, g, j)
            ex = self.att_s(g, b, kcur, kprev, qj, Bt4[:, boff:boff + 2, :], ltok, extra if j == 0 else None)
            pend.append((j, b, ex, vcur, vprev))
            if len(pend) > 1:
                jj, bb, exx, vc, vp = pend.pop(0)
                last_pv = self.att_pv(g, jj, bb, exx, vc, vp)
                acc_toks.append(self.od_new)
        while pend:
            jj, bb, exx, vc, vp = pend.pop(0)
            last_pv = self.att_pv(g, jj, bb, exx, vc, vp)
            acc_toks.append(self.od_new)
        self.od_last = acc_toks
        self.ht_free[s] = [last_pv, (self.P["dve"], self.P["dve"].count), (self.P["act"], self.P["act"].count)]
        if g == 2:
            r1 = self.cop("dve", lambda e: e.reciprocal(out=self.Rcp, in_=self.Dacc), waits=[acc_toks, self.u_free])
            r2 = self.cop("dve", lambda e: e.tensor_tensor(out=self.YB[:, h, :], in0=self.Oacc, in1=self.Rcp, op=ALU.mult), waits=[r1])
            self.od_last = [r2]
            self.u_free = r2

    def attention(self, kv_tok, kv_recv, stores_done, ya_done):
        att_pre = [ya_done] + self.latest()
        order = [(g, h) for h in range(8) for g in range(3)]
        self.ht_free = [None, None]
        self.sb_free = [None, None]
        self.pt_free = [None, None]
        self.sps_free = [self.sa_free, self.sa_free]
        self.ops_free = [None, None]
        self.unit = 0
        self.od_last = None
        lw = [kv_tok, stores_done, att_pre]
        loads = {0: self.att_load(0, order[0], kv_recv, lw)}
        for i, (g, h) in enumerate(order):
            if i + 1 < len(order):
                loads[i + 1] = self.att_load(i + 1, order[i + 1], kv_recv, lw)
            s, ltok = loads.pop(i)
            self.att_head(g, h, s, ltok, att_pre if i == 0 else None)
        self.sa_free = [(self.P["dve"], self.P["dve"].count), (self.P["act"], self.P["act"].count)]
        self.acc_free[0] = [self.acc_free[0], self.sa_free]

    def layer(self, l):
        pb = l * PL
        w_in = self.w_in[l]
        mainA, haloA = self.srcA()
        C_AH, C_AB, C_AC, C_Q, C_K, C_V, C_GA, C_GB = 0, 1024, 2048, 3072, 6144, 9216, 12288, 16384
        flag = self.par(P_FLAG)

        def wcol(base, c):
            return w_in[:, base + c * 128: base + (c + 1) * 128]

        nA = self.norm(pb + P_GMIX)
        aw = [nA]
        if self.level < 1:
            return

        kst_toks = []
        kv_toks = []
        for hd in range(NHEAD):
            def epi(accv, hslot, mm, hd=hd):
                t = self.st_i % 3
                self.st_i += 1
                S_ = self.ST[t]
                ev = self.cop("act", lambda e: e.activation(out=S_[:, 0:T], in_=accv, func=AF.Identity), waits=[mm, self.st_free[t]])
                dst = self.ks[hd // 8][(hd % 8) * 128:(hd % 8 + 1) * 128, :]
                st = self.op("sp", lambda e: e.dma_start(out=dst, in_=S_[:, 0:T]), waits=[ev], inc=self.s_st[t])
                self.st_free[t] = st
                kst_toks.append(st)
                return ev
            self.gemm_chunk(wcol(C_K, hd), NCH, mainA, None, aw, epi)
            if hd % 8 == 7:
                kv_toks.append(self.kv_collective(l % 2, hd // 8, kst_toks))
                kst_toks = []

        if self.level < 2:
            return
        pend = []
        vst_toks = []
        vst_i = [0]
        vst_free = [None, None]
        vt_free = [None]

        def emit_transposes(item):
            hd, S_, t, ev = item
            g = hd // 8

            def pe_fn(e):
                ins = None
                for j in range(8):
                    ins = e.transpose(self.VT[:, j, :], self.tsub(S_[:, 0:T], g, j), self.IDENT[:])
                return ins
            tp = self.cop("pe", pe_fn, waits=[ev, vt_free[0], self.setup_tok])
            self.st_free[t] = tp
            r = vst_i[0] % 2
            vst_i[0] += 1
            V_ = self.VST[r]
            cp = self.cop("dve", lambda e: e.tensor_copy(out=V_[:], in_=self.VT[:]), waits=[tp, vst_free[r]])
            vt_free[0] = cp
            dst = self.ks[3 + hd // 8][(hd % 8) * 128:(hd % 8 + 1) * 128, :]
            st = self.op("sp", lambda e: e.dma_start(out=dst, in_=V_[:].rearrange("p j d -> p (j d)")), waits=[cp], inc=self.s_vst[r])
            vst_free[r] = st
            vst_toks.append(st)
            if hd % 8 == 7:
                kv_toks.append(self.kv_collective(l % 2, 3 + hd // 8, list(vst_toks)))
                del vst_toks[:]

        for hd in range(NHEAD):
            def epi(accv, hslot, mm, hd=hd):
                t = self.st_i % 3
                self.st_i += 1
                S_ = self.ST[t]
                ev = self.cop("act", lambda e: e.activation(out=S_[:, 0:T], in_=accv, func=AF.Identity), waits=[mm, self.st_free[t]])
                self.st_free[t] = ev
                pend.append((hd, S_, t, ev))
                return ev
            self.gemm_chunk(wcol(C_V, hd), NCH, mainA, None, aw, epi)
            if len(pend) > 1:
                emit_transposes(pend.pop(0))
        while pend:
            emit_transposes(pend.pop(0))

        kv_recv = self.kr[l % 2]
        kv_tok = kv_toks
        if self.level < 4:
            return
        bfree = self.b_free
        for c in range(8):
            def epi(accv, hslot, mm, c=c):
                e1 = self.cop("act", lambda e: e.activation(out=self.AH[:, c, :], in_=accv, func=AF.Identity), waits=[mm, bfree])
                e2 = self.cop("act", lambda e: e.activation(out=self.AHh[:, c, :], in_=hslot, func=AF.Identity), waits=[mm, e1])
                return [e1, e2]
            self.gemm_chunk(wcol(C_AH, c), NCH, mainA, haloA, aw, epi)
        ah_done = (self.P["act"], self.P["act"].count)
        for c in range(8):
            def epi(accv, hslot, mm, c=c):
                cw = pb + P_CAW + 3 * c
                u1 = self.cop("dve", lambda e: e.tensor_tensor(out=self.U[:, 2:TH], in0=accv, in1=self.AH[:, c, :], op=ALU.mult),
                              waits=[mm, ah_done, self.u_free])
                u2 = self.cop("dve", lambda e: e.scalar_tensor_tensor(out=self.U[:, 0:2], in0=hslot, scalar=flag, in1=self.AHh[:, c, :],
                                                                      op0=ALU.mult, op1=ALU.mult), waits=[u1])
                c1 = self.cop("dve", lambda e: e.tensor_scalar(out=self.AH[:, c, :], in0=self.U[:, 2:TH], scalar1=self.par(cw + 2),
                                                               scalar2=self.par(pb + P_CAB + c), op0=ALU.mult, op1=ALU.add), waits=[u2])
                c2 = self.cop("dve", lambda e: e.scalar_tensor_tensor(out=self.AH[:, c, :], in0=self.U[:, 1:TH - 1], scalar=self.par(cw + 1),
                                                                      in1=self.AH[:, c, :], op0=ALU.mult, op1=ALU.add), waits=[c1])
                c3 = self.cop("dve", lambda e: e.scalar_tensor_tensor(out=self.AH[:, c, :], in0=self.U[:, 0:T], scalar=self.par(cw),
                                                                      in1=self.AH[:, c, :], op0=ALU.mult, op1=ALU.add), waits=[c2])
                self.u_free = c3
                return [u1, u2]
            self.gemm_chunk(wcol(C_AC, c), NCH, mainA, haloA, aw, epi)
        cu_done = (self.P["dve"], self.P["dve"].count)
        for c in range(8):
            def epi(accv, hslot, mm, c=c):
                return self.cop("dve", lambda e: e.tensor_tensor(out=self.YA[:, c, :], in0=accv, in1=self.AH[:, c, :], op=ALU.mult),
                                waits=[mm, cu_done])
            self.gemm_chunk(wcol(C_AB, c), NCH, mainA, None, aw, epi)
        ya_done = (self.P["dve"], self.P["dve"].count)

        if self.level < 5:
            return
        q_toks = []
        for hd in range(NHEAD):
            def epi(accv, hslot, mm, hd=hd):
                t = self.st_i % 3
                self.st_i += 1
                S_ = self.ST[t]
                ev = self.cop("act", lambda e: e.activation(out=S_[:, 0:T], in_=accv, func=AF.Identity, scale=128.0 ** -0.5),
                              waits=[mm, self.st_free[t]])
                st = self.op("sp", lambda e: e.dma_start(out=self.q_scr[hd * 128:(hd + 1) * 128, :], in_=S_[:, 0:T]),
                             waits=[ev], inc=self.s_st[t])
                self.st_free[t] = st
                q_toks.append(st)
                return ev
            self.gemm_chunk(wcol(C_Q, hd), NCH, mainA, None, aw, epi)
        stores_done = [(s, s.count) for s in self.s_st + self.s_vst if s.count > 0]

        if self.level < 6:
            return
        self.attention(kv_tok, kv_recv, stores_done, ya_done)
        att_done = self.latest()
        if self.level == 6:
            self.op("sp", lambda e: e.dma_start(out=self.dbg[:, 0:8 * T], in_=self.YA[:].rearrange("p c t -> p (c t)")), waits=att_done, inc=self.s_misc)
            self.op("sp", lambda e: e.dma_start(out=self.dbg[:, 8 * T:16 * T], in_=self.YB[:].rearrange("p c t -> p (c t)")), waits=att_done, inc=self.s_misc)
            self.op("sp", lambda e: None, waits=[(self.s_misc, self.s_misc.count)])

        if self.level < 7:
            return
        for hf in range(2):
            bw = att_done + ([self.b_free] if self.b_free else [])
            for c in range(16):
                mc = hf * 16 + c

                def epi(accv, hslot, mm, c=c):
                    return self.cop("act", lambda e: e.activation(out=self.B[:, c, :], in_=accv, func=AF.Sigmoid), waits=[mm] + bw)
                self.gemm_chunk(wcol(C_GA, mc), NCH, mainA, None, aw, epi)
            sg_done = (self.P["act"], self.P["act"].count)
            for c in range(16):
                mc = hf * 16 + c

                def epi(accv, hslot, mm, c=c):
                    return self.cop("dve", lambda e: e.tensor_tensor(out=self.B[:, c, :], in0=accv, in1=self.B[:, c, :], op=ALU.mult),
                                    waits=[mm, sg_done])
                self.gemm_chunk(self.w_ba[l][:, mc * 128:(mc + 1) * 128], 8, lambda k, h: self.YA[:, k, h * 512:(h + 1) * 512], None,
                                [ya_done], epi)
            for c in range(16):
                mc = hf * 16 + c
                t = self.st_i % 3
                self.st_i += 1
                S_ = self.ST[t]

                def epi(accv, hslot, mm, S_=S_, t=t):
                    ev = self.cop("act", lambda e: e.activation(out=S_[:, 0:T], in_=accv, func=AF.Sigmoid), waits=[mm, self.st_free[t]])
                    self._sg = ev
                    return ev
                self.gemm_chunk(wcol(C_GB, mc), NCH, mainA, None, aw, epi)
                sgt = self._sg

                def epi2(accv, hslot, mm, S_=S_, t=t, c=c, sgt=sgt):
                    m1 = self.cop("dve", lambda e: e.tensor_tensor(out=self.U[:, 0:T], in0=accv, in1=S_[:, 0:T], op=ALU.mult),
                                  waits=[mm, sgt, self.u_free])
                    self.st_free[t] = m1
                    m2 = self.cop("dve", lambda e: e.tensor_tensor(out=self.B[:, c, :], in0=self.U[:, 0:T], in1=self.B[:, c, :], op=ALU.add),
                                  waits=[m1])
                    self.u_free = m2
                    return m1
                self.gemm_chunk(self.w_bb[l][:, mc * 128:(mc + 1) * 128], 8, lambda k, h: self.YB[:, k, h * 512:(h + 1) * 512], None,
                                att_done, epi2)
            mg_done = (self.P["dve"], self.P["dve"].count)
            self.rmw_gemm(self.w_o[l][hf * 2048:(hf + 1) * 2048, :], 16, lambda k, h: self.B[:, k, h * 512:(h + 1) * 512],
                          [mg_done], last_pass=(hf == 1))
            self.b_free = (self.P["pe"], self.P["pe"].count)
        self.a_free = (self.P["pe"], self.P["pe"].count)
        if self.level < 8:
            return
        self.halo_exchange()
        if self.level < 9:
            return

        nF = self.norm(pb + P_GFFN)
        fw = [nF]
        w_up = self.w_up[l]
        for qt in range(4):
            for c in range(16):
                fc = qt * 16 + c

                def epi(accv, hslot, mm, c=c, fc=fc):
                    cw = pb + P_CFW + 3 * fc
                    a1 = self.cop("act", lambda e: e.activation(out=self.U[:, 2:TH], in_=accv, func=AF.Identity), waits=[mm, self.u_free])
                    a2 = self.cop("dve", lambda e: e.tensor_scalar(out=self.U[:, 0:2], in0=hslot, scalar1=flag, scalar2=None, op0=ALU.mult),
                                  waits=[mm, a1])
                    t = self.xc_i % 3
                    self.xc_i += 1
                    CV = self.XC[t]
                    c1 = self.cop("dve", lambda e: e.tensor_scalar(out=CV[:, 0:T], in0=self.U[:, 2:TH], scalar1=self.par(cw + 2),
                                                                   scalar2=self.par(pb + P_CFB + fc), op0=ALU.mult, op1=ALU.add),
                                  waits=[a1, a2, self.xc_free[t]])
                    c2 = self.cop("dve", lambda e: e.scalar_tensor_tensor(out=CV[:, 0:T], in0=self.U[:, 1:TH - 1], scalar=self.par(cw + 1),
                                                                          in1=CV[:, 0:T], op0=ALU.mult, op1=ALU.add), waits=[c1])
                    c3 = self.cop("dve", lambda e: e.scalar_tensor_tensor(out=CV[:, 0:T], in0=self.U[:, 0:T], scalar=self.par(cw),
                                                                          in1=CV[:, 0:T], op0=ALU.mult, op1=ALU.add), waits=[c2])
                    self.u_free = c3
                    ge = self.cop("act", lambda e: e.activation(out=self.B[:, c, :], in_=CV[:, 0:T], func=AF.Gelu_apprx_tanh),
                                  waits=[c3, self.b_free])
                    self.xc_free[t] = ge
                    return [a1, a2]
                self.gemm_chunk(w_up[:, fc * 128:(fc + 1) * 128], NCH, mainA, haloA, fw, epi)
            ge_done = (self.P["act"], self.P["act"].count)
            for c in range(16):
                fc = qt * 16 + c

                def epi(accv, hslot, mm, c=c):
                    return self.cop("dve", lambda e: e.tensor_tensor(out=self.B[:, c, :], in0=accv, in1=self.B[:, c, :], op=ALU.mult),
                                    waits=[mm, ge_done])
                self.gemm_chunk(w_up[:, DFF + fc * 128:DFF + (fc + 1) * 128], NCH, mainA, None, fw, epi)
            g_done = (self.P["dve"], self.P["dve"].count)
            self.rmw_gemm(self.w_down[l][qt * 2048:(qt + 1) * 2048, :], 16, lambda k, h: self.B[:, k, h * 512:(h + 1) * 512],
                          [g_done], last_pass=(qt == 3))
            self.b_free = (self.P["pe"], self.P["pe"].count)
        self.a_free = (self.P["pe"], self.P["pe"].count)
        if l + 1 < self.depth:
            self.halo_exchange()

    def final_norm(self):
        self.norm(P_GFIN, final=True)


def _t5_bucket(dist):
    max_exact = 16
    distf = np.maximum(dist, 1).astype(np.float32)
    large = max_exact + (np.log(distf / np.float32(max_exact)) / np.float32(math.log(2048 / max_exact))
                         * np.float32(32 - max_exact)).astype(np.int32)
    large = np.minimum(large, 31)
    return np.where(dist < max_exact, dist, large)


def _bias_tables(table, second_half):
    out = np.full((NHEAD, 128, 4, 128), NEG, np.float32)
    ki = np.arange(128)[:, None]
    qi = np.arange(128)[None, :]
    for g, d in enumerate(GROUP_DIL):
        bidx = _t5_bucket(np.arange(129) * d)
        for h in range(8):
            hd = 8 * g + h
            bd = table[bidx, hd].astype(np.float32)
            if g < 2:
                dl = qi - ki
                cur = np.where(dl >= 0, bd[np.clip(dl, 0, 128)], np.float32(NEG))
                dp = 128 + qi - ki
                prev = np.where(dp <= 128, bd[np.clip(dp, 0, 128)], np.float32(NEG))
                prevx = prev if second_half else np.full((128, 128), NEG, np.float32)
            else:
                same = ((qi - ki) % 2) == 0
                dl = (qi - ki) // 2
                cur = np.where(same & (dl >= 0), bd[np.clip(dl, 0, 128)], np.float32(NEG))
                prev = np.full((128, 128), NEG, np.float32)
                px = np.where(same, bd[np.clip(64 + dl, 0, 128)], np.float32(NEG))
                prevx = px if second_half else np.full((128, 128), NEG, np.float32)
            out[hd, :, 0, :] = cur
            out[hd, :, 1, :] = prev
            out[hd, :, 2, :] = cur
            out[hd, :, 3, :] = prevx
    return np.ascontiguousarray(out.reshape(NHEAD * 128, 512))


def _params(inp, second_half):
    P = np.zeros((128, NPAR), np.float32)

    def cm(v):
        return np.asarray(v, np.float32).reshape(-1, 128).T

    for l in range(DEPTH):
        b = l * PL
        P[:, b + P_GMIX:b + P_GMIX + 32] = cm(inp["norm_mix_g"][l])
        P[:, b + P_GFFN:b + P_GFFN + 32] = cm(inp["norm_ffn_g"][l])
        caw = np.asarray(inp["conv_a_w"][l], np.float32)
        P[:, b + P_CAW:b + P_CAW + 24] = caw.reshape(3, 8, 128).transpose(2, 1, 0).reshape(128, 24)
        P[:, b + P_CAB:b + P_CAB + 8] = cm(inp["conv_a_b"][l])
        cfw = np.asarray(inp["conv_f_w"][l], np.float32)
        P[:, b + P_CFW:b + P_CFW + 192] = cfw.reshape(3, 64, 128).transpose(2, 1, 0).reshape(128, 192)
        P[:, b + P_CFB:b + P_CFB + 64] = cm(inp["conv_f_b"][l])
    P[:, P_GFIN:P_GFIN + 32] = cm(inp["norm_final_g"])
    P[:, P_FLAG] = 1.0 if second_half else 0.0
    P[:, P_EPS] = EPS
    return P


_CACHE = {}


def kernel(x, rel_bias_table, norm_mix_g, w_in, conv_a_w, conv_a_b, w_branch_a, w_branch_b, w_o,
           norm_ffn_g, w_up, conv_f_w, conv_f_b, w_down, norm_final_g, _depth=DEPTH, _level=99):
    inp = dict(norm_mix_g=norm_mix_g, conv_a_w=conv_a_w, conv_a_b=conv_a_b, norm_ffn_g=norm_ffn_g,
               conv_f_w=conv_f_w, conv_f_b=conv_f_b, norm_final_g=norm_final_g)
    x = np.asarray(x, np.float32)
    table = np.asarray(rel_bias_table, np.float32)
    if (_depth, _level) not in _CACHE:
        _CACHE[(_depth, _level)] = Builder(_depth, _level).build()
    nc = _CACHE[(_depth, _level)]
    ident = np.eye(128, dtype=np.float32)
    dd = _depth
    shared = dict(ident=ident, w_in=np.asarray(w_in[:dd], np.float32))
    if _level >= 7:
        shared.update(w_branch_a=np.asarray(w_branch_a[:dd], np.float32), w_branch_b=np.asarray(w_branch_b[:dd], np.float32),
                      w_o=np.asarray(w_o[:dd], np.float32))
    if _level >= 9:
        shared.update(w_up=np.asarray(w_up[:dd], np.float32), w_down=np.asarray(w_down[:dd], np.float32))
    in_maps = []
    for core in range(N_CORES):
        b, half = core // 2, core % 2
        t0 = half * T
        xT = np.ascontiguousarray(x[b, t0:t0 + T, :].T)
        xh = np.zeros((128, 64), np.float32)
        if half == 1:
            hv = x[b, t0 - 2:t0, :]
            xh[:] = hv.reshape(2, 32, 128).transpose(2, 1, 0).reshape(128, 64)
        m = dict(xT=xT, xh=xh, params=_params(inp, half == 1), biasT=_bias_tables(table, half == 1))
        m.update(shared)
        in_maps.append(m)
    res = run_bass_kernel_spmd(nc, in_maps, core_ids=list(range(N_CORES)))
    if _level == 6:
        global _DBG
        _DBG = [np.asarray(res.results[c]["dbg"]).astype(np.float32) for c in range(N_CORES)]
    out = np.empty((4, 2 * T, D), np.float32)
    for core in range(N_CORES):
        b, half = core // 2, core % 2
        out[b, half * T:(half + 1) * T, :] = res.results[core]["outT"].T
    return out
```
